# Optimizing a Trainium2 kernel written in Bass

```python
import jax, jax.numpy as jnp
from jax import lax

D_MODEL = 1024
BATCH = 2
SEQ = 16384
DEPTH = 4

CHUNK = 64
Q_BLOCK = 128
HEAD_DIM = 64
ROPE_THETA = 10000.0
NORM_EPS = 1e-6
LRU_WIDTH = D_MODEL // 2
LRU_BLOCKS = 8
LRU_BLOCK_DIM = LRU_WIDTH // LRU_BLOCKS
LRU_CONV = 4
LRU_C = 8.0
DSA_HEADS = (D_MODEL // 2) // HEAD_DIM
IDX_HEADS = 8
IDX_DIM = 64
DSA_TOPK_MAX = 256
FOX_HEADS = (D_MODEL // 2) // HEAD_DIM
CONV_WIDTH = D_MODEL // 2
CONV_KERNEL = 31
FFN_HIDDEN = -(-8 * D_MODEL // (3 * 256)) * 256
N_EVEN = (DEPTH + 1) // 2
N_ODD = DEPTH // 2

EVEN_SPLITS = (LRU_WIDTH, LRU_WIDTH,
               DSA_HEADS * HEAD_DIM, DSA_HEADS * HEAD_DIM, DSA_HEADS * HEAD_DIM,
               IDX_HEADS * IDX_DIM, IDX_DIM, IDX_HEADS)
ODD_SPLITS = (FOX_HEADS * HEAD_DIM, FOX_HEADS * HEAD_DIM, FOX_HEADS * HEAD_DIM,
              FOX_HEADS, 2 * CONV_WIDTH)
EVEN_IN = sum(EVEN_SPLITS)
ODD_IN = sum(ODD_SPLITS)
EVEN_OUT = LRU_WIDTH + DSA_HEADS * HEAD_DIM
ODD_OUT = FOX_HEADS * HEAD_DIM + CONV_WIDTH

kernel_name = "hybrid_lru_dsa_fox_conformer_trunk"


def split_cols(z, sizes):
    offsets, acc = [], 0
    for s in sizes[:-1]:
        acc += s
        offsets.append(acc)
    return jnp.split(z, offsets, axis=-1)


def rms_norm(x, g):
    xf = x.astype(jnp.float32)
    y = xf * lax.rsqrt(jnp.mean(xf * xf, axis=-1, keepdims=True) + NORM_EPS)
    return (y * g.astype(jnp.float32)).astype(x.dtype)


def layer_norm(x, g, b):
    xf = x.astype(jnp.float32)
    mu = jnp.mean(xf, axis=-1, keepdims=True)
    var = jnp.mean(jnp.square(xf - mu), axis=-1, keepdims=True)
    y = (xf - mu) * lax.rsqrt(var + NORM_EPS)
    return (y * g.astype(jnp.float32) + b.astype(jnp.float32)).astype(x.dtype)


def rope(x, pos):
    d = x.shape[-1]
    inv = ROPE_THETA ** (-jnp.arange(0, d, 2, dtype=jnp.float32) / d)
    ang = pos[:, None] * inv[None, :]
    if x.ndim == 4:
        ang = ang[:, None, :]
    cos, sin = jnp.cos(ang), jnp.sin(ang)
    xf = x.astype(jnp.float32)
    x1, x2 = xf[..., : d // 2], xf[..., d // 2:]
    return jnp.concatenate([x1 * cos - x2 * sin, x1 * sin + x2 * cos], axis=-1).astype(x.dtype)


def causal_dwconv(x, w, b):
    K, C = w.shape
    y = lax.conv_general_dilated(x, w[:, None, :].astype(x.dtype), (1,), [(K - 1, 0)],
                                 dimension_numbers=('NWC', 'WIO', 'NWC'),
                                 feature_group_count=C)
    return y + b.astype(x.dtype)


def _lru_combine(c1, c2):
    a1, b1 = c1
    a2, b2 = c2
    return a1 * a2, a2 * b1 + b2


def rg_lru(x, w_r, b_r, w_i, b_i, lam):
    B, S, W = x.shape
    xf = x.astype(jnp.float32)
    xb = xf.reshape(B, S, LRU_BLOCKS, LRU_BLOCK_DIM)
    r = jax.nn.sigmoid(jnp.einsum('bsnd,nde->bsne', xb, w_r.astype(jnp.float32)).reshape(B, S, W)
                       + b_r.astype(jnp.float32))
    i = jax.nn.sigmoid(jnp.einsum('bsnd,nde->bsne', xb, w_i.astype(jnp.float32)).reshape(B, S, W)
                       + b_i.astype(jnp.float32))
    log_a = -LRU_C * r * jax.nn.softplus(-lam.astype(jnp.float32))
    a = jnp.exp(log_a)
    u = jnp.sqrt(-jnp.expm1(2.0 * log_a)) * (i * xf)
    _, h = lax.associative_scan(_lru_combine, (a, u), axis=1)
    return h.astype(x.dtype)


def dsa_attention(q, k, v, qi, ki, wi):
    B, S, H, Dh = q.shape
    topk = min(DSA_TOPK_MAX, S // 4)
    n_blk = S // Q_BLOCK
    key_chunk = jnp.arange(S) // CHUNK
    k_flat = k.reshape(B, S, H * Dh)
    v_flat = v.reshape(B, S, H * Dh)
    gather = jax.vmap(lambda arr, idx: jnp.take(arr, idx, axis=0))
    scale = Dh ** -0.5

    def block(blk):
        t0 = blk * Q_BLOCK
        qb = lax.dynamic_slice_in_dim(q, t0, Q_BLOCK, axis=1)
        qib = lax.dynamic_slice_in_dim(qi, t0, Q_BLOCK, axis=1)
        wib = lax.dynamic_slice_in_dim(wi, t0, Q_BLOCK, axis=1)
        q_chunk = (t0 + jnp.arange(Q_BLOCK)) // CHUNK
        rel = jax.nn.relu(jnp.einsum('bthd,bsd->bths', qib, ki).astype(jnp.float32))
        score = jnp.einsum('bth,bths->bts', wib.astype(jnp.float32), rel)
        vis = key_chunk[None, :] <= q_chunk[:, None]
        score = jnp.where(vis[None], score, -jnp.inf)
        _, idx = lax.top_k(score, topk)
        valid = (idx // CHUNK) <= q_chunk[None, :, None]
        flat = idx.reshape(B, Q_BLOCK * topk)
        ks = gather(k_flat, flat).reshape(B, Q_BLOCK, topk, H, Dh)
        vs = gather(v_flat, flat).reshape(B, Q_BLOCK, topk, H, Dh)
        logits = jnp.einsum('bthd,btkhd->bthk', qb, ks).astype(jnp.float32) * scale
        logits = jnp.where(valid[:, :, None, :], logits, -jnp.inf)
        p = jax.nn.softmax(logits, axis=-1)
        return jnp.einsum('bthk,btkhd->bthd', p.astype(v.dtype), vs)

    out = lax.map(block, jnp.arange(n_blk))
    return jnp.moveaxis(out, 0, 1).reshape(B, S, H * Dh)


def fox_attention(q, k, v, c):
    B, S, H, Dh = q.shape
    n_blk = S // Q_BLOCK
    cT = jnp.swapaxes(c, 1, 2)
    key_pos = jnp.arange(S)
    scale = Dh ** -0.5

    def block(blk):
        t0 = blk * Q_BLOCK
        qb = lax.dynamic_slice_in_dim(q, t0, Q_BLOCK, axis=1)
        cq = lax.dynamic_slice_in_dim(cT, t0, Q_BLOCK, axis=2)
        logits = jnp.einsum('bthd,bshd->bhts', qb, k).astype(jnp.float32) * scale
        logits = logits + cq[..., None] - cT[:, :, None, :]
        causal = key_pos[None, :] <= (t0 + jnp.arange(Q_BLOCK))[:, None]
        logits = jnp.where(causal[None, None], logits, -jnp.inf)
        p = jax.nn.softmax(logits, axis=-1)
        return jnp.einsum('bhts,bshd->bthd', p.astype(v.dtype), v)

    out = lax.map(block, jnp.arange(n_blk))
    return jnp.moveaxis(out, 0, 1).reshape(B, S, H * Dh)


def even_mixer(h, w_in, conv_w, conv_b, w_r, b_r, w_i, b_i, lam, q_g, k_g, w_out, pos):
    B, S, _ = h.shape
    z = h @ w_in
    xa, ga, q, k, v, qi, ki, wi = split_cols(z, EVEN_SPLITS)
    xa = causal_dwconv(xa, conv_w, conv_b)
    ya = rg_lru(xa, w_r, b_r, w_i, b_i, lam) * jax.nn.gelu(ga, approximate=True)
    q = rope(rms_norm(q.reshape(B, S, DSA_HEADS, HEAD_DIM), q_g), pos)
    k = rope(rms_norm(k.reshape(B, S, DSA_HEADS, HEAD_DIM), k_g), pos)
    v = v.reshape(B, S, DSA_HEADS, HEAD_DIM)
    qi = rope(qi.reshape(B, S, IDX_HEADS, IDX_DIM), pos)
    ki = rope(ki, pos)
    yb = dsa_attention(q, k, v, qi, ki, wi)
    return jnp.concatenate([ya, yb], axis=-1) @ w_out


def odd_mixer(h, w_in, b_f, q_g, k_g, conv_w, conv_b, ln_g, ln_b, w_out):
    B, S, _ = h.shape
    z = h @ w_in
    q, k, v, fl, u = split_cols(z, ODD_SPLITS)
    q = rms_norm(q.reshape(B, S, FOX_HEADS, HEAD_DIM), q_g)
    k = rms_norm(k.reshape(B, S, FOX_HEADS, HEAD_DIM), k_g)
    v = v.reshape(B, S, FOX_HEADS, HEAD_DIM)
    log_f = jax.nn.log_sigmoid(fl.astype(jnp.float32) + b_f.astype(jnp.float32))
    c = jnp.cumsum(log_f, axis=1)
    yc = fox_attention(q, k, v, c)
    ua, ug = jnp.split(u, 2, axis=-1)
    yd = causal_dwconv(ua * jax.nn.sigmoid(ug), conv_w, conv_b)
    yd = jax.nn.silu(layer_norm(yd, ln_g, ln_b))
    return jnp.concatenate([yc, yd], axis=-1) @ w_out


def swiglu(h, w_gu, w_down):
    g, u = jnp.split(h @ w_gu, 2, axis=-1)
    return (jax.nn.silu(g) * u) @ w_down


def setup_inputs(seed: int = 0) -> dict:
    key = jax.random.key(seed)
    ks = jax.random.split(key, 32)

    def nrm(k, shape, scale):
        return jax.random.normal(k, shape, jnp.float32) * scale

    out_scale = (2.0 * DEPTH) ** -0.5
    u = jax.random.uniform(ks[9], (N_EVEN, LRU_WIDTH), jnp.float32, 0.9, 0.999)
    a = u ** (1.0 / LRU_C)
    lam = jnp.log(a) - jnp.log1p(-a)
    return {
        "x": nrm(ks[0], (BATCH, SEQ, D_MODEL), 1.0),
        "norm_mix": 1.0 + nrm(ks[1], (DEPTH, D_MODEL), 0.02),
        "norm_ffn": 1.0 + nrm(ks[2], (DEPTH, D_MODEL), 0.02),
        "ev_w_in": nrm(ks[3], (N_EVEN, D_MODEL, EVEN_IN), D_MODEL ** -0.5),
        "ev_conv_w": nrm(ks[4], (N_EVEN, LRU_CONV, LRU_WIDTH), LRU_CONV ** -0.5),
        "ev_conv_b": nrm(ks[5], (N_EVEN, LRU_WIDTH), 0.02),
        "ev_w_r": nrm(ks[6], (N_EVEN, LRU_BLOCKS, LRU_BLOCK_DIM, LRU_BLOCK_DIM), LRU_BLOCK_DIM ** -0.5),
        "ev_b_r": nrm(ks[7], (N_EVEN, LRU_WIDTH), 0.02),
        "ev_w_i": nrm(ks[8], (N_EVEN, LRU_BLOCKS, LRU_BLOCK_DIM, LRU_BLOCK_DIM), LRU_BLOCK_DIM ** -0.5),
        "ev_b_i": nrm(ks[10], (N_EVEN, LRU_WIDTH), 0.02),
        "ev_lam": lam,
        "ev_q_norm": 1.0 + nrm(ks[11], (N_EVEN, HEAD_DIM), 0.02),
        "ev_k_norm": 1.0 + nrm(ks[12], (N_EVEN, HEAD_DIM), 0.02),
        "ev_w_out": nrm(ks[13], (N_EVEN, EVEN_OUT, D_MODEL), EVEN_OUT ** -0.5 * out_scale),
        "od_w_in": nrm(ks[14], (N_ODD, D_MODEL, ODD_IN), D_MODEL ** -0.5),
        "od_b_f": jax.random.uniform(ks[15], (N_ODD, FOX_HEADS), jnp.float32, 1.0, 5.0),
        "od_q_norm": 1.0 + nrm(ks[16], (N_ODD, HEAD_DIM), 0.02),
        "od_k_norm": 1.0 + nrm(ks[17], (N_ODD, HEAD_DIM), 0.02),
        "od_conv_w": nrm(ks[18], (N_ODD, CONV_KERNEL, CONV_WIDTH), CONV_KERNEL ** -0.5),
        "od_conv_b": nrm(ks[19], (N_ODD, CONV_WIDTH), 0.02),
        "od_ln_g": 1.0 + nrm(ks[20], (N_ODD, CONV_WIDTH), 0.02),
        "od_ln_b": nrm(ks[21], (N_ODD, CONV_WIDTH), 0.02),
        "od_w_out": nrm(ks[22], (N_ODD, ODD_OUT, D_MODEL), ODD_OUT ** -0.5 * out_scale),
        "ffn_w_gu": nrm(ks[23], (DEPTH, D_MODEL, 2 * FFN_HIDDEN), D_MODEL ** -0.5),
        "ffn_w_down": nrm(ks[24], (DEPTH, FFN_HIDDEN, D_MODEL), FFN_HIDDEN ** -0.5 * out_scale),
    }


def reference(x, norm_mix, norm_ffn,
              ev_w_in, ev_conv_w, ev_conv_b, ev_w_r, ev_b_r, ev_w_i, ev_b_i, ev_lam,
              ev_q_norm, ev_k_norm, ev_w_out,
              od_w_in, od_b_f, od_q_norm, od_k_norm, od_conv_w, od_conv_b, od_ln_g, od_ln_b,
              od_w_out, ffn_w_gu, ffn_w_down):
    S = x.shape[1]
    pos = jnp.arange(S, dtype=jnp.float32)
    for l in range(DEPTH):
        j = l // 2
        h = rms_norm(x, norm_mix[l])
        if l % 2 == 0:
            y = even_mixer(h, ev_w_in[j], ev_conv_w[j], ev_conv_b[j], ev_w_r[j], ev_b_r[j],
                           ev_w_i[j], ev_b_i[j], ev_lam[j], ev_q_norm[j], ev_k_norm[j],
                           ev_w_out[j], pos)
        else:
            y = odd_mixer(h, od_w_in[j], od_b_f[j], od_q_norm[j], od_k_norm[j],
                          od_conv_w[j], od_conv_b[j], od_ln_g[j], od_ln_b[j], od_w_out[j])
        x = x + y
        x = x + swiglu(rms_norm(x, norm_ffn[l]), ffn_w_gu[l], ffn_w_down[l])
    return x
```

```python
import numpy as np
from contextlib import ExitStack
import concourse.bass as bass
import concourse.mybir as mybir
from concourse.bass_utils import run_bass_kernel_spmd

F32 = mybir.dt.float32
BF16 = mybir.dt.bfloat16
ALU = mybir.AluOpType
AF = mybir.ActivationFunctionType
AX = mybir.AxisListType

NCORES = 8
EPS = 1e-6


class Prog:
    ENG = ("pe", "dve", "act", "pool", "sp")
    KDMA = 8

    def __init__(self):
        self.nc = bass.Bass("TRN2", target_bir_lowering=False)
        self.es = ExitStack()
        self.streams = {e: [] for e in self.ENG}
        self.seq = {e: 0 for e in self.ENG}
        self.seen = {e: {} for e in self.ENG}
        self.lastw = {}
        self.readers = {}
        self.dma_n = {e: 0 for e in self.ENG}
        self.dma_cnt = {}
        self.semh = {}
        self.out_events = []
        nc = self.nc
        for e in self.ENG:
            self.semh[e] = self.es.enter_context(nc.semaphore("s_" + e))
        for q in ("sp", "act", "pool"):
            for s in range(self.KDMA):
                self.semh[("dma", q, s)] = self.es.enter_context(nc.semaphore("d_%s%d" % (q, s)))

    def dram(self, name, shape, dt, kind):
        return self.nc.dram_tensor(name, list(shape), dt, kind=kind).ap()

    def sbuf(self, name, shape, dt):
        return self.es.enter_context(self.nc.sbuf_tensor("sb_" + name, list(shape), dt))

    def psum(self, name, shape, dt):
        return self.es.enter_context(self.nc.psum_tensor("ps_" + name, list(shape), dt))

    def _deps(self, reads, writes):
        evs = []
        for b in reads:
            if b in self.lastw:
                evs.append(self.lastw[b])
        for b in writes:
            if b in self.lastw:
                evs.append(self.lastw[b])
            evs.extend(self.readers.get(b, {}).items())
        return evs

    def _commit(self, ev, reads, writes):
        for b in writes:
            self.lastw[b] = ev
            self.readers[b] = {}
        for b in reads:
            if b not in writes:
                d = self.readers.setdefault(b, {})
                if d.get(ev[0], 0) < ev[1]:
                    d[ev[0]] = ev[1]

    def _waits(self, eng, evs, skip_self):
        need = {}
        for k, v in evs:
            if skip_self and k == eng:
                continue
            if self.seen[eng].get(k, 0) >= v:
                continue
            if need.get(k, 0) < v:
                need[k] = v
        for k, v in need.items():
            self.seen[eng][k] = v
        return list(need.items())

    def op(self, eng, fn, reads=(), writes=(), self_sync=True):
        evs = self._deps(reads, writes)
        waits = self._waits(eng, evs, not self_sync)
        self.seq[eng] += 1
        ev = (eng, self.seq[eng])
        self._commit(ev, reads, writes)
        self.streams[eng].append((waits, fn, eng, 1))

    def dma(self, q, out, in_, reads=(), writes=(), is_output=False):
        n = self.dma_n[q]
        self.dma_n[q] += 1
        key = ("dma", q, n % self.KDMA)
        cnt = self.dma_cnt.get(key, 0)
        evs = self._deps(reads, writes)
        if cnt > 0:
            evs.append((key, 16 * cnt))
        waits = self._waits(q, evs, False)
        self.dma_cnt[key] = cnt + 1
        ev = (key, 16 * (cnt + 1))
        self._commit(ev, reads, writes)
        self.streams[q].append((waits, lambda e: e.dma_start(out=out, in_=in_), key, 16))
        if is_output:
            self.out_events.append(ev)

    def finish(self):
        evs = list(self.out_events)
        for key, cnt in self.dma_cnt.items():
            evs.append((key, 16 * cnt))
        for e in ("pe", "dve", "act", "pool"):
            if self.seq[e]:
                evs.append((e, self.seq[e]))
        waits = self._waits("sp", evs, False)
        self.streams["sp"].append((waits, None, None, 0))
        nc = self.nc
        semh = self.semh
        streams = self.streams

        def mk(name):
            def body(e):
                for waits, fn, ik, iv in streams[name]:
                    for k, v in waits:
                        e.wait_ge(semh[k], v)
                    if fn is not None:
                        fn(e).then_inc(semh[ik], iv)
            return body

        with nc.Block() as block:
            block.tensor(mk("pe"))
            block.vector(mk("dve"))
            block.scalar(mk("act"))
            block.gpsimd(mk("pool"))
            block.sync(mk("sp"))
        self.es.close()
        return nc


def run(prog_nc, in_maps):
    res = run_bass_kernel_spmd(prog_nc, in_maps, core_ids=list(range(NCORES)))
    return res.results


def V(P, eng, name, *args, reads=(), writes=(), self_sync=True, **kw):
    P.op(eng, lambda e: getattr(e, name)(*args, **kw), reads, writes, self_sync)


S_LEN = 16384
TOKC = 4096
NT = TOKC // 128
FFH = 2816


def bc_last(ap2d, n):
    p, h = ap2d.shape
    return ap2d.unsqueeze(2).to_broadcast([p, h, n])


class TokLaunch:
    def __init__(self):
        P = self.P = Prog()
        self.wbf = P.sbuf("wbf", [128, 45056], BF16)
        self.wst = [P.sbuf("wst%d" % i, [128, 1408], F32) for i in range(2)]
        self.wsti = 0
        self.ident = P.sbuf("ident", [128, 128], BF16)
        identf = P.sbuf("identf", [128, 128], F32)
        self.psT = [P.psum("psT%d" % i, [128, 1024], BF16) for i in range(2)]
        self.psz = [P.psum("psz%d" % i, [128, 512], F32) for i in range(4)]
        self.pszi = 0
        self.psTi = 0
        self.ab = [P.sbuf("ab%d" % i, [128, FFH], BF16) for i in range(2)]
        self.aT = [P.sbuf("aT%d" % i, [128, FFH], BF16) for i in range(2)]
        self.xt = [P.sbuf("xt%d" % i, [128, 1024], F32) for i in range(2)]
        self.junk = P.sbuf("junk", [128, 1024], BF16)
        self.st = [P.sbuf("st%d" % i, [128, 16], F32) for i in range(2)]
        self.ob = [P.sbuf("ob%d" % i, [128, 512], F32) for i in range(4)]
        self.obi = 0
        self.big = [P.sbuf("big%d" % i, [128, 5632], F32) for i in range(1)]
        self.cnt = 0
        V(P, "pool", "memset", identf[:], 0.0, writes=["identf"])
        V(P, "pool", "affine_select", out=identf[:], in_=identf[:], pattern=[[-1, 128]],
          compare_op=ALU.not_equal, fill=1.0, base=0, channel_multiplier=1, reads=["identf"], writes=["identf"])
        V(P, "dve", "tensor_copy", out=self.ident[:], in_=identf[:], reads=["identf"], writes=["ident"])

    def load_w(self, w_ap, KC, NCOL, gcol=None):
        P = self.P
        for k in range(KC):
            for c0 in range(0, NCOL, 1408):
                c1 = min(NCOL, c0 + 1408)
                s = self.wsti % 2
                self.wsti += 1
                P.dma("sp", self.wst[s][:, 0:c1 - c0], w_ap[k * 128:(k + 1) * 128, c0:c1], writes=[("wst", s)])
                dst = self.wbf[:, k * NCOL + c0:k * NCOL + c1]
                if gcol is not None:
                    V(P, "dve", "tensor_scalar", out=dst, in0=self.wst[s][:, 0:c1 - c0], scalar1=gcol[:, k:k + 1],
                      scalar2=None, op0=ALU.mult, reads=[("wst", s), "gcol"], writes=[("wbf", k)])
                else:
                    V(P, "dve", "tensor_copy", out=dst, in_=self.wst[s][:, 0:c1 - c0], reads=[("wst", s)], writes=[("wbf", k)])

    def rstd_from_ss(self, ss_ap, key, n, eps=EPS):
        P = self.P
        V(P, "dve", "tensor_scalar", out=ss_ap, in0=ss_ap, scalar1=1.0 / n, scalar2=eps, op0=ALU.mult, op1=ALU.add, reads=[key], writes=[key])
        V(P, "act", "activation", out=ss_ap, in_=ss_ap, func=AF.Sqrt, reads=[key], writes=[key])
        V(P, "dve", "reciprocal", out=ss_ap, in_=ss_ap, reads=[key], writes=[key])

    def a_rms(self, src, dkey):
        def f(i, s):
            P = self.P
            P.dma("sp", self.xt[s][:], src[i * 128:(i + 1) * 128, :], reads=[(dkey, i)], writes=[("xt", s)])
            V(P, "act", "activation", out=self.junk[:], in_=self.xt[s][:], func=AF.Square, accum_out=self.st[s][:, 0:1],
              reads=[("xt", s)], writes=["junk", ("st", s)])
            self.rstd_from_ss(self.st[s][:, 0:1], ("st", s), 1024)
            V(P, "pool", "tensor_copy", out=self.ab[s][:, 0:1024], in_=self.xt[s][:], reads=[("xt", s)], writes=[("ab", s)])
        return f

    def a_swiglu(self, src, dkey):
        def f(i, s):
            P = self.P
            big = self.big[0]
            P.dma("sp", big[:], src[i * 128:(i + 1) * 128, :], reads=[(dkey, i)], writes=["big"])
            V(P, "act", "activation", out=big[:, 0:FFH], in_=big[:, 0:FFH], func=AF.Silu, reads=["big"], writes=["big"])
            V(P, "dve", "tensor_tensor", out=self.ab[s][:, 0:FFH], in0=big[:, 0:FFH], in1=big[:, FFH:2 * FFH], op=ALU.mult,
              reads=["big"], writes=[("ab", s)])
        return f

    def linear(self, KC, NCOL, a_fn, epi_fn, tile_done=None):
        P = self.P
        ncg = (NCOL + 511) // 512
        for i in range(NT):
            s = self.cnt % 2
            self.cnt += 1
            a_fn(i, s)
            for k0 in range(0, KC, 8):
                nk = min(8, KC - k0)
                pb = self.psTi % 2
                self.psTi += 1
                for k in range(k0, k0 + nk):
                    V(P, "pe", "transpose", out=self.psT[pb][:, (k - k0) * 128:(k - k0 + 1) * 128], in_=self.ab[s][:, k * 128:(k + 1) * 128],
                      identity=self.ident[:], reads=[("ab", s), "ident"], writes=[("psT", pb)], self_sync=False)
                V(P, "dve" if (pb == 0) else "act", "tensor_copy" if pb == 0 else "copy", out=self.aT[s][:, k0 * 128:(k0 + nk) * 128], in_=self.psT[pb][:, 0:nk * 128],
                  reads=[("psT", pb)], writes=[("aT", s)])
            for cg in range(ncg):
                c0 = cg * 512
                c1 = min(NCOL, c0 + 512)
                pz = self.pszi % 4
                self.pszi += 1
                for k in range(KC):
                    V(P, "pe", "matmul", self.psz[pz][:, 0:c1 - c0], lhsT=self.aT[s][:, k * 128:(k + 1) * 128], rhs=self.wbf[:, k * NCOL + c0:k * NCOL + c1],
                      start=(k == 0), stop=(k == KC - 1), reads=[("aT", s), ("wbf", k)], writes=[("psz", pz)], self_sync=False)
                epi_fn(i, s, c0, c1, self.psz[pz][:, 0:c1 - c0], ("psz", pz))
            if tile_done is not None:
                tile_done(i, s)

    def epi_store_scaled(self, dst, dkey):
        def f(i, s, c0, c1, ps, pkey):
            P = self.P
            o = self.obi % 4
            self.obi += 1
            V(P, "act", "activation", out=self.ob[o][:, 0:c1 - c0], in_=ps, func=AF.Copy, scale=self.st[s][:, 0:1], reads=[pkey, ("st", s)], writes=[("ob", o)])
            P.dma("sp", dst[i * 128:(i + 1) * 128, c0:c1], self.ob[o][:, 0:c1 - c0], reads=[("ob", o)], writes=[(dkey, i)], is_output=True)
        return f

    def epi_resid(self, dst, dkey, res_tile_fn):
        def f(i, s, c0, c1, ps, pkey):
            P = self.P
            o = self.obi % 4
            self.obi += 1
            rap, rkey = res_tile_fn(i, s)
            V(P, "dve", "tensor_tensor", out=self.ob[o][:, 0:c1 - c0], in0=ps, in1=rap[:, c0:c1], op=ALU.add, reads=[pkey, rkey], writes=[("ob", o)])
            P.dma("sp", dst[i * 128:(i + 1) * 128, c0:c1], self.ob[o][:, 0:c1 - c0], reads=[("ob", o)], writes=[(dkey, i)], is_output=True)
        return f


def _tok_extra(T):
    P = T.P
    if hasattr(T, "zt"):
        return
    T.zt = T.big[0][:, 0:3144]
    T.tmp = [P.sbuf("tmp%d" % i, [128, 512], F32) for i in range(4)]
    T.qn = P.sbuf("qn", [128, 512], F32)
    T.obf = [P.sbuf("obf%d" % i, [128, 512], BF16) for i in range(4)]
    T.obfi = 0
    T.cs = [P.sbuf("cs%d" % i, [128, 512], F32) for i in range(2)]
    T.s8 = [P.sbuf("s8%d" % i, [128, 16], F32) for i in range(2)]
    T.bnst = P.sbuf("bnst", [128, 8], F32)
    T.draw = P.sbuf("draw", [128, 520], F32)


def _rope(T, src3, dst3, H, csbuf, keys_r, key_w):
    P = T.P
    cos = csbuf[:, 0:H * 32].rearrange("p (h d) -> p h d", h=H)
    sin = csbuf[:, 256:256 + H * 32].rearrange("p (h d) -> p h d", h=H)
    x1 = src3[:, :, 0:32]
    x2 = src3[:, :, 32:64]
    t = [T.tmp[j][:, 0:H * 32].rearrange("p (h d) -> p h d", h=H) for j in range(4)]
    V(P, "dve", "tensor_tensor", out=t[0], in0=x1, in1=cos, op=ALU.mult, reads=keys_r + ["cs"], writes=["tmp0"])
    V(P, "dve", "tensor_tensor", out=t[1], in0=x2, in1=sin, op=ALU.mult, reads=keys_r + ["cs"], writes=["tmp1"])
    V(P, "pool", "tensor_tensor", out=t[2], in0=x1, in1=sin, op=ALU.mult, reads=keys_r + ["cs"], writes=["tmp2"])
    V(P, "pool", "tensor_tensor", out=t[3], in0=x2, in1=cos, op=ALU.mult, reads=keys_r + ["cs"], writes=["tmp3"])
    V(P, "dve", "tensor_tensor", out=dst3[:, :, 0:32], in0=t[0], in1=t[1], op=ALU.subtract, reads=["tmp0", "tmp1"], writes=[key_w])
    V(P, "pool", "tensor_tensor", out=dst3[:, :, 32:64], in0=t[2], in1=t[3], op=ALU.add, reads=["tmp2", "tmp3"], writes=[key_w])


def _headnorm(T, src2, dst2, g_tile, key_r, key_w, s8ap, s8key):
    P = T.P
    V(P, "dve", "tensor_tensor", out=T.tmp[0][:], in0=src2, in1=src2, op=ALU.mult, reads=[key_r], writes=["tmp0"])
    V(P, "dve", "tensor_reduce", out=s8ap, in_=T.tmp[0][:].rearrange("p (h d) -> p h d", h=8), axis=AX.X, op=ALU.add, reads=["tmp0"], writes=[s8key])
    T.rstd_from_ss(s8ap, s8key, 64)
    V(P, "dve", "tensor_tensor", out=T.tmp[1][:].rearrange("p (h d) -> p h d", h=8), in0=src2.rearrange("p (h d) -> p h d", h=8),
      in1=bc_last(s8ap, 64), op=ALU.mult, reads=[key_r, s8key], writes=["tmp1"])
    V(P, "dve", "tensor_tensor", out=dst2, in0=T.tmp[1][:], in1=g_tile, op=ALU.mult, reads=["tmp1", "gains"], writes=[key_w])


def _out_bf(T, dst, i, src_ap, src_key, dkey, w=512):
    P = T.P
    P.dma("sp", dst[i * 128:(i + 1) * 128, :], src_ap, reads=[src_key], writes=[(dkey, i)], is_output=True)


def prep_even(T, outs, cs_dram, gq, gk):
    _tok_extra(T)
    P = T.P

    def f(i, s):
        zt = T.zt
        c = i % 2
        P.dma("pool", T.cs[c][:], cs_dram[i * 128:(i + 1) * 128, :], writes=["cs"])
        P.dma("sp", outs["xa"][i * 128:(i + 1) * 128, :], zt[:, 0:512], reads=["big"], writes=[("o_xa", i)], is_output=True)
        P.dma("sp", outs["ga"][i * 128:(i + 1) * 128, :], zt[:, 512:1024], reads=["big"], writes=[("o_ga", i)], is_output=True)
        P.dma("sp", outs["wi"][i * 128:(i + 1) * 128, :], zt[:, 3136:3144], reads=["big"], writes=[("o_wi", i)], is_output=True)
        for nm, c0, g in (("q", 1024, gq), ("k", 1536, gk)):
            _headnorm(T, zt[:, c0:c0 + 512], T.qn[:], g[:], "big", "qn", T.s8[0][:, 0:8], "s80")
            o = T.obfi % 4
            T.obfi += 1
            _rope(T, T.qn[:].rearrange("p (h d) -> p h d", h=8), T.obf[o][:].rearrange("p (h d) -> p h d", h=8), 8, T.cs[c], ["qn"], ("obf", o))
            _out_bf(T, outs[nm], i, T.obf[o][:], ("obf", o), "o_" + nm)
        o = T.obfi % 4
        T.obfi += 1
        V(P, "act", "copy", out=T.obf[o][:], in_=zt[:, 2048:2560], reads=["big"], writes=[("obf", o)])
        _out_bf(T, outs["v"], i, T.obf[o][:], ("obf", o), "o_v")
        _rope(T, zt[:, 2560:3072].rearrange("p (h d) -> p h d", h=8), T.qn[:].rearrange("p (h d) -> p h d", h=8), 8, T.cs[c], ["big"], "qn")
        o = T.obfi % 4
        T.obfi += 1
        V(P, "dve", "tensor_tensor", out=T.obf[o][:].rearrange("p (h d) -> p h d", h=8), in0=T.qn[:].rearrange("p (h d) -> p h d", h=8),
          in1=bc_last(zt[:, 3136:3144], 64), op=ALU.mult, reads=["qn", "big"], writes=[("obf", o)])
        _out_bf(T, outs["qiw"], i, T.obf[o][:], ("obf", o), "o_qiw")
        o = T.obfi % 4
        T.obfi += 1
        _rope(T, zt[:, 3072:3136].rearrange("p (h d) -> p h d", h=1), T.obf[o][:, 0:64].rearrange("p (h d) -> p h d", h=1), 1, T.cs[c], ["big"], ("obf", o))
        P.dma("sp", outs["ki"][i * 128:(i + 1) * 128, :], T.obf[o][:, 0:64], reads=[("obf", o)], writes=[("o_ki", i)], is_output=True)
    return f


def prep_odd(T, outs, gq, gk):
    _tok_extra(T)
    P = T.P

    def f(i, s):
        zt = T.zt
        for nm, c0, g in (("q", 0, gq), ("k", 512, gk)):
            o = T.obfi % 4
            T.obfi += 1
            _headnorm(T, zt[:, c0:c0 + 512], T.obf[o][:], g[:], "big", ("obf", o), T.s8[0][:, 0:8], "s80")
            _out_bf(T, outs[nm], i, T.obf[o][:], ("obf", o), "o_" + nm)
        o = T.obfi % 4
        T.obfi += 1
        V(P, "act", "copy", out=T.obf[o][:], in_=zt[:, 1024:1536], reads=["big"], writes=[("obf", o)])
        _out_bf(T, outs["v"], i, T.obf[o][:], ("obf", o), "o_v")
        P.dma("sp", outs["fl"][i * 128:(i + 1) * 128, :], zt[:, 1536:1544], reads=["big"], writes=[("o_fl", i)], is_output=True)
        V(P, "act", "activation", out=T.tmp[0][:], in_=zt[:, 2056:2568], func=AF.Sigmoid, reads=["big"], writes=["tmp0"])
        V(P, "dve", "tensor_tensor", out=T.qn[:], in0=zt[:, 1544:2056], in1=T.tmp[0][:], op=ALU.mult, reads=["big", "tmp0"], writes=["qn"])
        P.dma("sp", outs["glu"][i * 128:(i + 1) * 128, :], T.qn[:], reads=["qn"], writes=[("o_glu", i)], is_output=True)
    return f


def epi_ztile(T):
    def f(i, s, c0, c1, ps, pkey):
        V(T.P, "act", "activation", out=T.zt[:, c0:c1], in_=ps, func=AF.Copy, scale=T.st[s][:, 0:1], reads=[pkey, ("st", s)], writes=["big"])
    return f


def a_mix(T, x_src, xkey, att_src, other_src, odd, lng=None, lnb=None):
    _tok_extra(T)
    P = T.P

    def f(i, s):
        P.dma("sp", T.xt[s][:], x_src[i * 128:(i + 1) * 128, :], reads=[(xkey, i)], writes=[("xt", s)])
        P.dma("pool", T.draw[:], att_src[i * 128:(i + 1) * 128, :], writes=["draw"])
        d3 = T.draw[:].rearrange("p (h d) -> p h d", h=8)
        V(P, "dve", "reciprocal", out=T.s8[1][:, 0:8].unsqueeze(2), in_=d3[:, :, 64:65], reads=["draw"], writes=["s81"])
        acol = 512 if not odd else 0
        ocol = 0 if not odd else 512
        V(P, "dve", "tensor_tensor", out=T.ab[s][:, acol:acol + 512].rearrange("p (h d) -> p h d", h=8), in0=d3[:, :, 0:64],
          in1=bc_last(T.s8[1][:, 0:8], 64), op=ALU.mult, reads=["draw", "s81"], writes=[("ab", s)])
        P.dma("pool", T.qn[:], other_src[i * 128:(i + 1) * 128, :], writes=["qn"])
        if not odd:
            V(P, "pool", "tensor_copy", out=T.ab[s][:, ocol:ocol + 512], in_=T.qn[:], reads=["qn"], writes=[("ab", s)])
        else:
            V(P, "dve", "bn_stats", out=T.bnst[:, 0:6], in_=T.qn[:], reads=["qn"], writes=["bnst"])
            V(P, "dve", "bn_aggr", out=T.s8[0][:, 8:10], in_=T.bnst[:, 0:6], reads=["bnst"], writes=["s80"])
            V(P, "dve", "tensor_scalar", out=T.s8[0][:, 9:10], in0=T.s8[0][:, 9:10], scalar1=EPS, scalar2=None, op0=ALU.add, reads=["s80"], writes=["s80"])
            V(P, "act", "activation", out=T.s8[0][:, 9:10], in_=T.s8[0][:, 9:10], func=AF.Sqrt, reads=["s80"], writes=["s80"])
            V(P, "dve", "reciprocal", out=T.s8[0][:, 9:10], in_=T.s8[0][:, 9:10], reads=["s80"], writes=["s80"])
            V(P, "dve", "tensor_scalar", out=T.tmp[0][:], in0=T.qn[:], scalar1=T.s8[0][:, 8:9], scalar2=T.s8[0][:, 9:10], op0=ALU.subtract, op1=ALU.mult,
              reads=["qn", "s80"], writes=["tmp0"])
            V(P, "dve", "tensor_tensor", out=T.tmp[1][:], in0=T.tmp[0][:], in1=lng[:], op=ALU.mult, reads=["tmp0", "gains"], writes=["tmp1"])
            V(P, "pool", "tensor_tensor", out=T.tmp[2][:], in0=T.tmp[1][:], in1=lnb[:], op=ALU.add, reads=["tmp1", "gains"], writes=["tmp2"])
            V(P, "act", "activation", out=T.ab[s][:, ocol:ocol + 512], in_=T.tmp[2][:], func=AF.Silu, reads=["tmp2"], writes=[("ab", s)])
    return f


def a_swiglu_res(T, gu_src, gkey, x_src, xkey):
    base = T.a_swiglu(gu_src, gkey)

    def f(i, s):
        T.P.dma("pool", T.xt[s][:], x_src[i * 128:(i + 1) * 128, :], reads=[(xkey, i)], writes=[("xt", s)])
        base(i, s)
    return f


def build_tok(first, last, nxt_odd, cur_odd):
    T = TokLaunch()
    P = T.P
    _tok_extra(T)
    I = lambda n, shp, dt=F32: P.dram(n, shp, dt, "ExternalInput")
    O = lambda n, shp, dt=F32: P.dram(n, shp, dt, "ExternalOutput")
    x_in = I("x", [TOKC, 1024])
    gcol = P.sbuf("gcol", [128, 8], F32)
    gains = [P.sbuf("gain%d" % j, [128, 512], F32) for j in range(4)]
    x_cur, xk = x_in, "xin"
    if not first:
        att = I("att", [TOKC, 520])
        oth = I("oth", [TOKC, 512])
        w_out = I("w_out", [1024, 1024])
        w_gu = I("w_gu", [1024, 2 * FFH])
        w_down = I("w_down", [FFH, 1024])
        g_ffn = I("g_ffn", [128, 8])
        x1 = P.dram("x1", [TOKC, 1024], F32, "Internal")
        gu = P.dram("gu", [TOKC, 2 * FFH], F32, "Internal")
        x2 = O("x2", [TOKC, 1024])
        if cur_odd:
            lng = I("lng", [128, 512])
            lnb = I("lnb", [128, 512])
            P.dma("sp", gains[2][:], lng, writes=["gains"])
            P.dma("sp", gains[3][:], lnb, writes=["gains"])
        T.load_w(w_out, 8, 1024)
        T.linear(8, 1024, a_mix(T, x_in, "xin", att, oth, cur_odd, gains[2], gains[3]),
                 T.epi_resid(x1, "x1", lambda i, s: (T.xt[s], ("xt", s))))
        P.dma("sp", gcol[:], g_ffn, reads=[("wbf", k) for k in range(8)], writes=["gcol"])
        T.load_w(w_gu, 8, 2 * FFH, gcol)
        T.linear(8, 2 * FFH, T.a_rms(x1, "x1"), T.epi_store_scaled(gu, "gu"))
        T.load_w(w_down, 22, 1024)
        T.linear(22, 1024, a_swiglu_res(T, gu, "gu", x1, "x1"), T.epi_resid(x2, "x2", lambda i, s: (T.xt[s], ("xt", s))))
        x_cur, xk = x2, "x2"
    if not last:
        NC_IN = 2568 if nxt_odd else 3144
        w_in = I("w_in", [1024, NC_IN])
        g_mix = I("g_mix", [128, 8])
        gq_d = I("gq", [128, 512])
        gk_d = I("gk", [128, 512])
        P.dma("sp", gains[0][:], gq_d, writes=["gains"])
        P.dma("sp", gains[1][:], gk_d, writes=["gains"])
        P.dma("sp", gcol[:], g_mix, reads=[("wbf", k) for k in range(8)], writes=["gcol"])
        T.load_w(w_in, 8, NC_IN, gcol)
        if nxt_odd:
            outs = {"q": O("o_q", [TOKC, 512], BF16), "k": O("o_k", [TOKC, 512], BF16), "v": O("o_v", [TOKC, 512], BF16),
                    "fl": O("o_fl", [TOKC, 8]), "glu": O("o_glu", [TOKC, 512])}
            hook = prep_odd(T, outs, gains[0], gains[1])
        else:
            cs_d = I("cs", [TOKC, 512])
            outs = {"q": O("o_q", [TOKC, 512], BF16), "k": O("o_k", [TOKC, 512], BF16), "v": O("o_v", [TOKC, 512], BF16),
                    "qiw": O("o_qiw", [TOKC, 512], BF16), "ki": O("o_ki", [TOKC, 64], BF16), "wi": O("o_wi", [TOKC, 8]),
                    "xa": O("o_xa", [TOKC, 512]), "ga": O("o_ga", [TOKC, 512])}
            hook = prep_even(T, outs, cs_d, gains[0], gains[1])
        T.linear(8, NC_IN, T.a_rms(x_cur, xk), epi_ztile(T), tile_done=hook)
    return P.finish()


def build_modd():
    P = Prog()
    S = S_LEN
    I = lambda n, shp, dt=F32: P.dram(n, shp, dt, "ExternalInput")
    O = lambda n, shp, dt=F32: P.dram(n, shp, dt, "ExternalOutput")
    qT = I("qT", [2, 64, S], BF16)
    kT = I("kT", [2, 64, S], BF16)
    vt = I("vt", [2, S, 64], BF16)
    fl = I("fl", [2, S])
    bfd = I("bf", [128, 1])
    glu = I("glu", [128, S])
    cwd = I("cw_in", [128, 31])
    cbd = I("cb_in", [128, 1])
    mkd = I("masks_in", [128, 4 * 512])
    oT = O("oT", [2, 65, S])
    cv = O("cv", [128, S])

    QA = P.sbuf("QA", [70, S], BF16)
    KA = P.sbuf("KA", [70, S], BF16)
    Vs = P.sbuf("Vs", [128, 128 * 65], BF16)
    cw = P.sbuf("cwk", [1, S], F32)
    stg = [P.sbuf("stg%d" % j, [1, 1024], BF16) for j in range(6)]
    masks = P.sbuf("masks", [128, 2048], F32)
    dtmp = P.sbuf("dtmp", [128, 512], F32)
    cwt = P.sbuf("cwt", [128, 31], F32)
    cbt = P.sbuf("cbt", [128, 1], F32)
    bft = P.sbuf("bft", [128, 1], F32)
    one1 = P.sbuf("one1", [1, 1], F32)
    cbuf = P.sbuf("cbuf", [128, 2048 + 30], F32)
    cacc = P.sbuf("cacc", [128, 2048], F32)
    pT = [P.sbuf("pT%d" % j, [128, 512], BF16) for j in range(3)]
    oS = [P.sbuf("oS%d" % j, [65, 512], F32) for j in range(2)]
    ps_s = [P.psum("ps_s%d" % j, [128, 512], F32) for j in range(3)]
    ps_o = [P.psum("ps_o%d" % j, [65, 512], F32) for j in range(2)]

    P.dma("sp", masks[:], mkd, writes=["masks"])
    P.dma("sp", cwt[:], cwd, writes=["cwt"])
    P.dma("sp", cbt[:], cbd, writes=["cbt"])
    P.dma("sp", bft[:], bfd, writes=["bft"])
    V(P, "dve", "tensor_scalar", out=bft[:], in0=bft[:], scalar1=-1.0, scalar2=None, op0=ALU.mult, reads=["bft"], writes=["bft"])
    V(P, "pool", "memset", one1[:], 1.0, writes=["one1"])

    CW = 2048
    for n in range(S // CW):
        if n == 0:
            V(P, "pool", "memset", cbuf[:, 0:30], 0.0, writes=["cbuf"])
            P.dma("pool", cbuf[:, 30:CW + 30], glu[:, 0:CW], writes=["cbuf"], reads=["cbuf"])
        else:
            P.dma("pool", cbuf[:], glu[:, n * CW - 30:(n + 1) * CW], writes=["cbuf"])
        V(P, "dve", "tensor_scalar", out=cacc[:], in0=cbuf[:, 0:CW], scalar1=cwt[:, 0:1], scalar2=cbt[:, 0:1], op0=ALU.mult, op1=ALU.add,
          reads=["cbuf", "cwt", "cbt"], writes=["cacc"])
        for j in range(1, 31):
            V(P, "dve", "scalar_tensor_tensor", out=cacc[:], in0=cbuf[:, j:j + CW], scalar=cwt[:, j:j + 1], in1=cacc[:], op0=ALU.mult, op1=ALU.add,
              reads=["cbuf", "cwt", "cacc"], writes=["cacc"])
        P.dma("pool", cv[:, n * CW:(n + 1) * CW], cacc[:], reads=["cacc"], writes=[("cv", n)], is_output=True)

    cnt_s = 0
    cnt_o = 0
    for b in range(2):
        P.dma("sp", QA[0:64, :], qT[b], writes=["QA"])
        P.dma("sp", KA[0:64, :], kT[b], writes=["KA"])
        V(P, "pool", "memset", QA[64:70, :], 1.0, writes=["QA"], reads=["QA"])
        V(P, "pool", "memset", KA[64:70, :], 1.0, writes=["KA"], reads=["KA"])
        V(P, "pool", "memset", Vs[:], 1.0, writes=["Vs"])
        P.dma("sp", Vs[:].rearrange("p (n d) -> p n d", d=65)[:, :, 0:64], vt[b].rearrange("(n p) d -> p n d", p=128), reads=["Vs"], writes=["Vs"])
        P.dma("sp", cw[:], fl[b:b + 1, :], writes=["cw"])
        V(P, "act", "activation", out=cw[:], in_=cw[:], func=AF.Exp, scale=-1.0, bias=bft[0:1, :], reads=["cw", "bft"], writes=["cw"])
        V(P, "act", "activation", out=cw[:], in_=cw[:], func=AF.Ln, bias=1.0, reads=["cw"], writes=["cw"])
        V(P, "dve", "tensor_tensor_scan", out=cw[:], data0=one1[:, 0:1].to_broadcast([1, S]), data1=cw[:], initial=0.0, op0=ALU.mult, op1=ALU.subtract,
          reads=["cw", "one1"], writes=["cw"])
        V(P, "dve", "tensor_scalar", out=cw[:], in0=cw[:], scalar1=8.0, scalar2=None, op0=ALU.mult, reads=["cw"], writes=["cw"])
        for ch in range(16):
            sl = slice(ch * 1024, (ch + 1) * 1024)
            for j in range(3):
                V(P, "dve", "tensor_copy", out=stg[j][:], in_=cw[:, sl], reads=["cw"], writes=[("stg", j)])
                V(P, "dve", "tensor_scalar", out=stg[3 + j][:], in0=stg[j][:], scalar1=-1.0, scalar2=None, op0=ALU.mult, reads=[("stg", j)], writes=[("stg", 3 + j)])
                if j < 2:
                    V(P, "dve", "tensor_tensor", out=cw[:, sl], in0=cw[:, sl], in1=stg[j][:], op=ALU.subtract, reads=["cw", ("stg", j)], writes=["cw"])
                P.dma("pool", QA[64 + j:65 + j, sl], stg[j][:], reads=[("stg", j), "QA"], writes=["QA"])
                P.dma("pool", KA[67 + j:68 + j, sl], stg[3 + j][:], reads=[("stg", 3 + j), "KA"], writes=["KA"])
        for qg in range(32):
            po = cnt_o % 2
            cnt_o += 1
            nkb = 4 * qg + 4
            for kb in range(nkb):
                psb = cnt_s % 3
                cnt_s += 1
                V(P, "pe", "matmul", ps_s[psb][:], lhsT=KA[:, kb * 128:(kb + 1) * 128], rhs=QA[:, qg * 512:(qg + 1) * 512], start=True, stop=True,
                  reads=["KA", "QA"], writes=[("ps_s", psb)], self_sync=False)
                if kb >= 4 * qg:
                    j = kb - 4 * qg
                    V(P, "dve", "tensor_tensor", out=dtmp[:], in0=ps_s[psb][:], in1=masks[:, j * 512:(j + 1) * 512], op=ALU.add,
                      reads=[("ps_s", psb), "masks"], writes=["dtmp"])
                    V(P, "act", "activation", out=pT[psb][:], in_=dtmp[:], func=AF.Exp, scale=0.125, reads=["dtmp"], writes=[("pT", psb)])
                else:
                    V(P, "act", "activation", out=pT[psb][:], in_=ps_s[psb][:], func=AF.Exp, scale=0.125, reads=[("ps_s", psb)], writes=[("pT", psb)])
                V(P, "pe", "matmul", ps_o[po][:], lhsT=Vs[:, kb * 65:(kb + 1) * 65], rhs=pT[psb][:], start=(kb == 0), stop=(kb == nkb - 1),
                  reads=["Vs", ("pT", psb)], writes=[("ps_o", po)], self_sync=False)
            V(P, "dve", "tensor_copy", out=oS[po][:], in_=ps_o[po][:], reads=[("ps_o", po)], writes=[("oS", po)])
            P.dma("sp", oT[b][:, qg * 512:(qg + 1) * 512], oS[po][:], reads=[("oS", po)], writes=[("oT", b, qg)], is_output=True)
    return P.finish()


def build_meven():
    P = Prog()
    S = S_LEN
    I = lambda n, shp, dt=F32: P.dram(n, shp, dt, "ExternalInput")
    O = lambda n, shp, dt=F32: P.dram(n, shp, dt, "ExternalOutput")
    xa = I("xa", [128, S])
    ga = I("ga", [128, S])
    cw4 = I("cw4", [128, 4])
    cb4 = I("cb4", [128, 1])
    wr_d = I("wr", [128, 128])
    wi_d = I("wig", [128, 128])
    brd = I("br", [128, 1])
    bid = I("bi", [128, 1])
    lamd = I("lam", [128, 1])
    ya = O("ya", [128, S])
    qiT_d = I("qiT", [32, 64, 1024], BF16)
    qT_d = I("qT", [32, 64, 1024], BF16)
    wsl_d = I("wsl", [32, 128, 8])
    kiT_d = I("kiT", [2, 64, S], BF16)
    kT_d = I("kT", [2, 64, 8 * S], BF16)
    va_d = I("vaug", [2, S, 520], BF16)
    vis_d = I("vis", [128, 1024])
    dsa = O("dsa", [32, 128, 520])

    sc = P.sbuf("sc", [128, S], F32)
    mb = P.sbuf("mb", [128, S], BF16)
    kiT = P.sbuf("kiT", [64, S], BF16)
    kch = [P.sbuf("kch%d" % j, [64, 8 * 512], BF16) for j in range(2)]
    vch = [P.sbuf("vch%d" % j, [128, 4 * 520], BF16) for j in range(2)]
    tt = [P.sbuf("tt%d" % j, [128, 512], F32) for j in range(2)]
    pT = [P.sbuf("pT%d" % j, [128, 1024], BF16) for j in range(2)]
    mT = [P.sbuf("mT%d" % j, [128, 128], BF16) for j in range(2)]
    qiT = [P.sbuf("qiT%d" % j, [64, 1024], BF16) for j in range(2)]
    qT = [P.sbuf("qT%d" % j, [64, 1024], BF16) for j in range(2)]
    wsl = [P.sbuf("wsl%d" % j, [128, 8], F32) for j in range(2)]
    hi8 = [P.sbuf("hi8%d" % j, [128, 8], F32) for j in range(2)]
    lo8 = [P.sbuf("lo8%d" % j, [128, 8], F32) for j in range(2)]
    vis = P.sbuf("vis", [128, 1024], F32)
    bs = P.sbuf("bs", [128, 16], F32)
    osb = [P.sbuf("osb%d" % j, [128, 520], F32) for j in range(2)]
    ident = P.sbuf("ident", [128, 128], BF16)
    identf = P.sbuf("identf", [128, 128], F32)
    wrb = P.sbuf("wrb", [128, 128], BF16)
    wib = P.sbuf("wib", [128, 128], BF16)
    wst = P.sbuf("wstg", [128, 128], F32)
    sm = P.sbuf("sm", [128, 16], F32)
    xcb = P.sbuf("xcb", [128, 1024], BF16)
    ps_i = [P.psum("ps_i%d" % j, [128, 512], F32) for j in range(2)]
    ps_m = P.psum("ps_m", [128, 128], BF16)
    ps_s = [P.psum("ps_s%d" % j, [128, 512], F32) for j in range(2)]
    ps_o0 = P.psum("ps_o0", [128, 455], F32)
    ps_o1 = P.psum("ps_o1", [128, 65], F32)

    V(P, "pool", "memset", identf[:], 0.0, writes=["identf"])
    V(P, "pool", "affine_select", out=identf[:], in_=identf[:], pattern=[[-1, 128]], compare_op=ALU.not_equal, fill=1.0, base=0,
      channel_multiplier=1, reads=["identf"], writes=["identf"])
    V(P, "dve", "tensor_copy", out=ident[:], in_=identf[:], reads=["identf"], writes=["ident"])

    CW = 1024
    L = {nm: sc[:, j * 1040:j * 1040 + 1040] for j, nm in enumerate(["xb", "xc", "r", "i", "a", "a2", "u", "g", "t", "h"])}
    lk = ["l_" + k for k in L]
    P.dma("sp", sm[:, 0:4], cw4, writes=["sm"])
    P.dma("sp", sm[:, 4:5], cb4, writes=["sm"], reads=["sm"])
    P.dma("sp", sm[:, 5:6], brd, writes=["sm"], reads=["sm"])
    P.dma("sp", sm[:, 6:7], bid, writes=["sm"], reads=["sm"])
    P.dma("sp", sm[:, 7:8], lamd, writes=["sm"], reads=["sm"])
    P.dma("sp", wst[:], wr_d, writes=["wst"])
    V(P, "dve", "tensor_copy", out=wrb[:], in_=wst[:], reads=["wst"], writes=["wrb"])
    P.dma("sp", wst[:], wi_d, writes=["wst"], reads=["wst"])
    V(P, "dve", "tensor_copy", out=wib[:], in_=wst[:], reads=["wst"], writes=["wib"])
    V(P, "act", "activation", out=sm[:, 8:9], in_=sm[:, 7:8], func=AF.Exp, scale=-1.0, reads=["sm"], writes=["sm"])
    V(P, "act", "activation", out=sm[:, 8:9], in_=sm[:, 8:9], func=AF.Ln, bias=1.0, reads=["sm"], writes=["sm"])
    V(P, "dve", "tensor_scalar", out=sm[:, 9:10], in0=sm[:, 8:9], scalar1=-16.0, scalar2=None, op0=ALU.mult, reads=["sm"], writes=["sm"])
    V(P, "dve", "tensor_scalar", out=sm[:, 8:9], in0=sm[:, 8:9], scalar1=-8.0, scalar2=None, op0=ALU.mult, reads=["sm"], writes=["sm"])
    for n in range(S // CW):
        c0 = n * CW
        xb = L["xb"]
        if n == 0:
            V(P, "pool", "memset", xb[:, 0:3], 0.0, writes=["l_xb"])
            P.dma("sp", xb[:, 3:3 + CW], xa[:, 0:CW], reads=["l_xb"], writes=["l_xb"])
        else:
            P.dma("sp", xb[:, 0:3 + CW], xa[:, c0 - 3:c0 + CW], writes=["l_xb"])
        P.dma("pool", L["g"][:, 0:CW], ga[:, c0:c0 + CW], writes=["l_g"])
        xc = L["xc"][:, 0:CW]
        V(P, "dve", "tensor_scalar", out=xc, in0=xb[:, 0:CW], scalar1=sm[:, 0:1], scalar2=sm[:, 4:5], op0=ALU.mult, op1=ALU.add,
          reads=["l_xb", "sm"], writes=["l_xc"])
        for j in range(1, 4):
            V(P, "dve", "scalar_tensor_tensor", out=xc, in0=xb[:, j:j + CW], scalar=sm[:, j:j + 1], in1=xc, op0=ALU.mult, op1=ALU.add,
              reads=["l_xb", "sm", "l_xc"], writes=["l_xc"])
        V(P, "pool", "tensor_copy", out=xcb[:], in_=xc, reads=["l_xc"], writes=["xcb"])
        for sub in range(CW // 512):
            sl = slice(sub * 512, (sub + 1) * 512)
            V(P, "pe", "matmul", ps_i[0][:], lhsT=wrb[:], rhs=xcb[:, sl], start=True, stop=True, reads=["wrb", "xcb"], writes=[("ps_i", 0)], self_sync=False)
            V(P, "pe", "matmul", ps_i[1][:], lhsT=wib[:], rhs=xcb[:, sl], start=True, stop=True, reads=["wib", "xcb"], writes=[("ps_i", 1)], self_sync=False)
            V(P, "act", "activation", out=L["r"][:, sl], in_=ps_i[0][:], func=AF.Sigmoid, bias=sm[:, 5:6], reads=[("ps_i", 0), "sm"], writes=["l_r"])
            V(P, "act", "activation", out=L["i"][:, sl], in_=ps_i[1][:], func=AF.Sigmoid, bias=sm[:, 6:7], reads=[("ps_i", 1), "sm"], writes=["l_i"])
        r = L["r"][:, 0:CW]
        V(P, "act", "activation", out=L["a"][:, 0:CW], in_=r, func=AF.Exp, scale=sm[:, 8:9], reads=["l_r", "sm"], writes=["l_a"])
        V(P, "act", "activation", out=L["a2"][:, 0:CW], in_=r, func=AF.Exp, scale=sm[:, 9:10], reads=["l_r", "sm"], writes=["l_a2"])
        V(P, "dve", "tensor_scalar", out=L["a2"][:, 0:CW], in0=L["a2"][:, 0:CW], scalar1=-1.0, scalar2=1.0, op0=ALU.mult, op1=ALU.add, reads=["l_a2"], writes=["l_a2"])
        V(P, "act", "activation", out=L["a2"][:, 0:CW], in_=L["a2"][:, 0:CW], func=AF.Sqrt, reads=["l_a2"], writes=["l_a2"])
        V(P, "dve", "tensor_tensor", out=L["u"][:, 0:CW], in0=L["i"][:, 0:CW], in1=xc, op=ALU.mult, reads=["l_i", "l_xc"], writes=["l_u"])
        V(P, "dve", "tensor_tensor", out=L["u"][:, 0:CW], in0=L["u"][:, 0:CW], in1=L["a2"][:, 0:CW], op=ALU.mult, reads=["l_u", "l_a2"], writes=["l_u"])
        if n > 0:
            V(P, "dve", "tensor_copy", out=sm[:, 10:11], in_=L["h"][:, CW - 1:CW], reads=["l_h"], writes=["sm10"])
        V(P, "dve", "tensor_tensor_scan", out=L["h"][:, 0:CW], data0=L["a"][:, 0:CW], data1=L["u"][:, 0:CW], initial=(0.0 if n == 0 else sm[:, 10:11]),
          op0=ALU.mult, op1=ALU.add, reads=["l_a", "l_u", "sm10"], writes=["l_h"])
        g = L["g"][:, 0:CW]
        t = L["t"][:, 0:CW]
        V(P, "pool", "tensor_tensor", out=t, in0=g, in1=g, op=ALU.mult, reads=["l_g"], writes=["l_t"])
        V(P, "pool", "tensor_scalar", out=t, in0=t, scalar1=0.044715, scalar2=1.0, op0=ALU.mult, op1=ALU.add, reads=["l_t"], writes=["l_t"])
        V(P, "pool", "tensor_tensor", out=t, in0=t, in1=g, op=ALU.mult, reads=["l_t", "l_g"], writes=["l_t"])
        V(P, "act", "activation", out=t, in_=t, func=AF.Sigmoid, scale=1.5957691216057308, reads=["l_t"], writes=["l_t"])
        V(P, "pool", "tensor_tensor", out=t, in0=t, in1=g, op=ALU.mult, reads=["l_t", "l_g"], writes=["l_t"])
        V(P, "dve", "tensor_tensor", out=t, in0=t, in1=L["h"][:, 0:CW], op=ALU.mult, reads=["l_t", "l_h"], writes=["l_t"])
        P.dma("pool", ya[:, c0:c0 + CW], t, reads=["l_t"], writes=[("ya", n)], is_output=True)
    V(P, "pool", "memset", sc[:, 0:1], 0.0, reads=lk, writes=["sc"] + lk)

    P.dma("sp", vis[:], vis_d, writes=["vis"])
    cs = 0
    for b in range(2):
        P.dma("sp", kiT[:], kiT_d[b], writes=["kiT"])
        kT3 = kT_d[b].rearrange("d (h s) -> d h s", h=8)
        for sp_ in range(16):
            slot = b * 16 + sp_
            N = 1024 * (sp_ + 1)
            z = slot % 2
            P.dma("sp", qiT[z][:], qiT_d[slot], writes=[("qiT", z)])
            P.dma("sp", qT[z][:], qT_d[slot], writes=[("qT", z)])
            P.dma("sp", wsl[z][:], wsl_d[slot], writes=[("wsl", z)])
            V(P, "dve", "tensor_scalar", out=hi8[z][:], in0=wsl[z][:], scalar1=0.0, scalar2=3.0e38, op0=ALU.is_gt, op1=ALU.mult, reads=[("wsl", z)], writes=[("hi8", z)])
            V(P, "dve", "tensor_scalar", out=lo8[z][:], in0=wsl[z][:], scalar1=0.0, scalar2=-3.0e38, op0=ALU.is_lt, op1=ALU.mult, reads=[("wsl", z)], writes=[("lo8", z)])
            for kt in range(N // 512):
                ksl = slice(kt * 512, (kt + 1) * 512)
                for h in range(8):
                    pb = h % 2
                    V(P, "pe", "matmul", ps_i[pb][:], lhsT=qiT[z][:, h * 128:(h + 1) * 128], rhs=kiT[:, ksl], start=True, stop=True,
                      reads=[("qiT", z), "kiT"], writes=[("ps_i", pb)], self_sync=False)
                    if h == 0:
                        V(P, "dve", "tensor_scalar", out=sc[:, ksl], in0=ps_i[pb][:], scalar1=hi8[z][:, h:h + 1], scalar2=lo8[z][:, h:h + 1], op0=ALU.min, op1=ALU.max,
                          reads=[("ps_i", pb), ("hi8", z), ("lo8", z)], writes=["sc"])
                    else:
                        V(P, "dve", "tensor_scalar", out=tt[pb][:], in0=ps_i[pb][:], scalar1=hi8[z][:, h:h + 1], scalar2=lo8[z][:, h:h + 1], op0=ALU.min, op1=ALU.max,
                          reads=[("ps_i", pb), ("hi8", z), ("lo8", z)], writes=[("tt", pb)])
                        V(P, "pool", "tensor_tensor", out=sc[:, ksl], in0=sc[:, ksl], in1=tt[pb][:], op=ALU.add, reads=["sc", ("tt", pb)], writes=["sc"])
            V(P, "dve", "tensor_reduce", out=bs[:, 1:2], in_=sc[:, 0:N], axis=AX.X, op=ALU.max, reads=["sc"], writes=["bs"])
            V(P, "dve", "tensor_reduce", out=bs[:, 0:1], in_=sc[:, 0:N], axis=AX.X, op=ALU.min, reads=["sc", "bs"], writes=["bs"])
            V(P, "dve", "tensor_scalar", out=bs[:, 1:2], in0=bs[:, 1:2], scalar1=1.0, scalar2=None, op0=ALU.add, reads=["bs"], writes=["bs"])
            V(P, "dve", "tensor_tensor", out=sc[:, N - 1024:N], in0=sc[:, N - 1024:N], in1=vis[:], op=ALU.add, reads=["sc", "vis"], writes=["sc"])
            for it in range(24):
                V(P, "dve", "tensor_tensor", out=bs[:, 2:3], in0=bs[:, 0:1], in1=bs[:, 1:2], op=ALU.add, reads=["bs"], writes=["bs"])
                V(P, "dve", "tensor_scalar", out=bs[:, 2:3], in0=bs[:, 2:3], scalar1=0.5, scalar2=None, op0=ALU.mult, reads=["bs"], writes=["bs"])
                V(P, "dve", "tensor_scalar", out=mb[:, 0:N], in0=sc[:, 0:N], scalar1=bs[:, 2:3], scalar2=None, op0=ALU.is_ge, op1=ALU.add, accum_out=bs[:, 3:4],
                  reads=["sc", "bs"], writes=["mb", "bs"])
                V(P, "dve", "tensor_scalar", out=bs[:, 4:5], in0=bs[:, 3:4], scalar1=255.5, scalar2=None, op0=ALU.is_gt, reads=["bs"], writes=["bs"])
                V(P, "dve", "tensor_tensor", out=bs[:, 5:6], in0=bs[:, 2:3], in1=bs[:, 0:1], op=ALU.subtract, reads=["bs"], writes=["bs"])
                V(P, "dve", "tensor_tensor", out=bs[:, 6:7], in0=bs[:, 1:2], in1=bs[:, 2:3], op=ALU.subtract, reads=["bs"], writes=["bs"])
                V(P, "dve", "scalar_tensor_tensor", out=bs[:, 0:1], in0=bs[:, 5:6], scalar=bs[:, 4:5], in1=bs[:, 0:1], op0=ALU.mult, op1=ALU.add, reads=["bs"], writes=["bs"])
                V(P, "dve", "scalar_tensor_tensor", out=bs[:, 1:2], in0=bs[:, 6:7], scalar=bs[:, 4:5], in1=bs[:, 2:3], op0=ALU.mult, op1=ALU.add, reads=["bs"], writes=["bs"])
            V(P, "dve", "tensor_scalar", out=mb[:, 0:N], in0=sc[:, 0:N], scalar1=bs[:, 0:1], scalar2=None, op0=ALU.is_ge, reads=["sc", "bs"], writes=["mb"])
            NKB = N // 128
            for kc in range(N // 512):
                y = cs % 2
                cs += 1
                P.dma("sp", kch[y][:].rearrange("d (h s) -> d h s", h=8), kT3[:, :, kc * 512:(kc + 1) * 512], writes=[("kch", y)])
                P.dma("pool", vch[y][:].rearrange("p (n c) -> p n c", c=520), va_d[b][kc * 512:(kc + 1) * 512, :].rearrange("(n p) c -> p n c", p=128), writes=[("vch", y)])
                for kk in range(4):
                    kb = kc * 4 + kk
                    w = kb % 2
                    V(P, "pe", "transpose", out=ps_m[:], in_=mb[:, kb * 128:(kb + 1) * 128], identity=ident[:], reads=["mb", "ident"], writes=["ps_m"], self_sync=False)
                    V(P, "act", "copy", out=mT[w][:], in_=ps_m[:], reads=["ps_m"], writes=[("mT", w)])
                    for h in range(8):
                        V(P, "pe", "matmul", ps_s[h // 4][:, (h % 4) * 128:(h % 4 + 1) * 128], lhsT=kch[y][:, h * 512 + kk * 128:h * 512 + (kk + 1) * 128],
                          rhs=qT[z][:, h * 128:(h + 1) * 128], start=True, stop=True, reads=[("kch", y), ("qT", z)], writes=[("ps_s", h // 4)], self_sync=False)
                    for hh in range(2):
                        V(P, "act", "activation", out=pT[w][:, hh * 512:(hh + 1) * 512], in_=ps_s[hh][:], func=AF.Exp, scale=0.125, reads=[("ps_s", hh)], writes=[("pT", w)])
                    V(P, "pool", "tensor_tensor", out=pT[w][:].rearrange("p (h q) -> p h q", h=8), in0=pT[w][:].rearrange("p (h q) -> p h q", h=8),
                      in1=mT[w][:].unsqueeze(1).to_broadcast([128, 8, 128]), op=ALU.mult, reads=[("pT", w), ("mT", w)], writes=[("pT", w)])
                    for h in range(8):
                        od = ps_o0[:, h * 65:(h + 1) * 65] if h < 7 else ps_o1[:, 0:65]
                        V(P, "pe", "matmul", od, lhsT=pT[w][:, h * 128:(h + 1) * 128], rhs=vch[y][:, kk * 520 + h * 65:kk * 520 + (h + 1) * 65],
                          start=True, stop=True, reads=[("pT", w), ("vch", y)], writes=["ps_o"], self_sync=False)
                    if kb == 0:
                        V(P, "dve", "tensor_copy", out=osb[z][:, 0:455], in_=ps_o0[:], reads=["ps_o", ("osb", z)], writes=[("osb", z)])
                        V(P, "dve", "tensor_copy", out=osb[z][:, 455:520], in_=ps_o1[:], reads=["ps_o", ("osb", z)], writes=[("osb", z)])
                    else:
                        V(P, "dve", "tensor_tensor", out=osb[z][:, 0:455], in0=osb[z][:, 0:455], in1=ps_o0[:], op=ALU.add, reads=["ps_o", ("osb", z)], writes=[("osb", z)])
                        V(P, "dve", "tensor_tensor", out=osb[z][:, 455:520], in0=osb[z][:, 455:520], in1=ps_o1[:], op=ALU.add, reads=["ps_o", ("osb", z)], writes=[("osb", z)])
            P.dma("sp", dsa[slot], osb[z][:], reads=[("osb", z)], writes=[("dsa", slot)], is_output=True)
    return P.finish()


def _gcol(g):
    return np.ascontiguousarray(np.asarray(g, np.float32).reshape(8, 128).T)


def _rows(v, n=128):
    v = np.asarray(v, np.float32).reshape(1, -1)
    return np.ascontiguousarray(np.broadcast_to(v, (n, v.shape[1])))


def _tile8(g64):
    return _rows(np.tile(np.asarray(g64, np.float32), 8))


def _cs_table(c):
    pos = ((c % 4) * TOKC + np.arange(TOKC)).astype(np.float32)
    inv = (np.float32(10000.0) ** (-np.arange(0, 64, 2, dtype=np.float32) / np.float32(64))).astype(np.float32)
    ang = (pos[:, None] * inv[None, :]).astype(np.float32)
    cos = np.tile(np.cos(ang).astype(np.float32), (1, 8))
    sin = np.tile(np.sin(ang).astype(np.float32), (1, 8))
    return np.ascontiguousarray(np.concatenate([cos, sin], 1))


def _diag_masks():
    m = np.zeros((128, 4, 512), np.float32)
    s = np.arange(128)[:, None]
    t = np.arange(512)[None, :]
    for j in range(4):
        qb = t // 128
        vis = (qb > j) | ((qb == j) & (s <= (t % 128)))
        m[:, j, :] = np.where(vis, 0.0, -240000.0)
    return np.ascontiguousarray(m.reshape(128, 2048))


def _vis(c):
    vis = np.zeros((128, 1024), np.float32)
    vis[0:64, c * 128 + 64:] = -1e30
    vis[64:128, c * 128 + 128:] = -1e30
    return vis


def _cat(res, key):
    return np.concatenate([np.asarray(r[key]) for r in res], 0)


def _shard(a, c):
    return np.ascontiguousarray(a[c * TOKC:(c + 1) * TOKC])


def _run_meven(o, j, P_):
    import ml_dtypes
    S = S_LEN
    q = o["q"].reshape(2, S, 8, 64)
    kk = o["k"].reshape(2, S, 8, 64)
    vv = o["v"].reshape(2, S, 8, 64)
    qiw = o["qiw"].reshape(2, S, 8, 64)
    kiT = np.ascontiguousarray(o["ki"].reshape(2, S, 64).transpose(0, 2, 1))
    kT = np.ascontiguousarray(kk.transpose(0, 3, 2, 1)).reshape(2, 64, 8 * S)
    vaug = np.concatenate([vv, np.ones((2, S, 8, 1), ml_dtypes.bfloat16)], -1).reshape(2, S, 520)
    wi = o["wi"].reshape(2, S, 8)
    xa = o["xa"].reshape(2, S, 8, 64)
    ga = o["ga"].reshape(2, S, 8, 64)
    ins = []
    for c in range(NCORES):
        cs_ = slice(c * 64, (c + 1) * 64)
        z = np.zeros((64, 64), np.float32)
        wr = np.block([[P_["ev_w_r"][j][c], z], [z, P_["ev_w_r"][j][c]]]).astype(np.float32)
        wg = np.block([[P_["ev_w_i"][j][c], z], [z, P_["ev_w_i"][j][c]]]).astype(np.float32)
        col2 = lambda v: np.ascontiguousarray(np.tile(np.asarray(v[cs_], np.float32), 2).reshape(128, 1))
        qiT = np.empty((32, 64, 1024), ml_dtypes.bfloat16)
        qT = np.empty((32, 64, 1024), ml_dtypes.bfloat16)
        wsl = np.empty((32, 128, 8), np.float32)
        for b in range(2):
            for sp_ in range(16):
                t0 = (8 * sp_ + c) * 128
                sl = b * 16 + sp_
                qiT[sl] = qiw[b, t0:t0 + 128].transpose(2, 1, 0).reshape(64, 1024)
                qT[sl] = q[b, t0:t0 + 128].transpose(2, 1, 0).reshape(64, 1024)
                wsl[sl] = wi[b, t0:t0 + 128]
        ins.append({
            "xa": np.ascontiguousarray(xa[:, :, c, :].transpose(0, 2, 1)).reshape(128, S),
            "ga": np.ascontiguousarray(ga[:, :, c, :].transpose(0, 2, 1)).reshape(128, S),
            "cw4": np.ascontiguousarray(np.tile(P_["ev_conv_w"][j][:, cs_].T, (2, 1))).astype(np.float32),
            "cb4": col2(P_["ev_conv_b"][j]), "wr": wr, "wig": wg, "br": col2(P_["ev_b_r"][j]), "bi": col2(P_["ev_b_i"][j]),
            "lam": col2(P_["ev_lam"][j]), "qiT": qiT, "qT": qT, "wsl": wsl, "kiT": kiT, "kT": kT, "vaug": vaug, "vis": _vis(c)})
    res = run(build_meven(), ins)
    ya = np.empty((2, S, 8, 64), np.float32)
    att = np.empty((2, S, 520), np.float32)
    for c in range(NCORES):
        ya[:, :, c, :] = np.asarray(res[c]["ya"]).reshape(2, 64, S).transpose(0, 2, 1)
        d = np.asarray(res[c]["dsa"])
        for b in range(2):
            for sp_ in range(16):
                t0 = (8 * sp_ + c) * 128
                att[b, t0:t0 + 128] = d[b * 16 + sp_]
    return att.reshape(2 * S, 520), ya.reshape(2 * S, 512)


def _run_modd(o, j, P_):
    S = S_LEN
    q = o["q"].reshape(2, S, 8, 64)
    kk = o["k"].reshape(2, S, 8, 64)
    vv = o["v"].reshape(2, S, 8, 64)
    fl = o["fl"].reshape(2, S, 8)
    glu = o["glu"].reshape(2, S, 8, 64)
    masks = _diag_masks()
    ins = []
    for c in range(NCORES):
        cs_ = slice(c * 64, (c + 1) * 64)
        ins.append({
            "qT": np.ascontiguousarray(q[:, :, c, :].transpose(0, 2, 1)), "kT": np.ascontiguousarray(kk[:, :, c, :].transpose(0, 2, 1)),
            "vt": np.ascontiguousarray(vv[:, :, c, :]), "fl": np.ascontiguousarray(fl[:, :, c]),
            "bf": np.full((128, 1), P_["od_b_f"][j][c], np.float32),
            "glu": np.ascontiguousarray(glu[:, :, c, :].transpose(0, 2, 1)).reshape(128, S),
            "cw_in": np.ascontiguousarray(np.tile(P_["od_conv_w"][j][:, cs_].T, (2, 1))).astype(np.float32),
            "cb_in": np.ascontiguousarray(np.tile(np.asarray(P_["od_conv_b"][j][cs_], np.float32), 2).reshape(128, 1)),
            "masks_in": masks})
    res = run(build_modd(), ins)
    att = np.empty((2, S, 8, 65), np.float32)
    cv = np.empty((2, S, 8, 64), np.float32)
    for c in range(NCORES):
        att[:, :, c, :] = np.asarray(res[c]["oT"]).transpose(0, 2, 1)
        cv[:, :, c, :] = np.asarray(res[c]["cv"]).reshape(2, 64, S).transpose(0, 2, 1)
    return att.reshape(2 * S, 520), cv.reshape(2 * S, 512)


DEBUG = None


def _dbg(name, val):
    if DEBUG is not None:
        DEBUG[name] = val


def kernel(**inp):
    P_ = {k: np.asarray(v) for k, v in inp.items()}
    x = np.ascontiguousarray(P_["x"].reshape(2 * S_LEN, 1024).astype(np.float32))

    def inproj_inputs(l):
        j = l // 2
        if l % 2 == 0:
            d = {"w_in": np.ascontiguousarray(P_["ev_w_in"][j]), "gq": _tile8(P_["ev_q_norm"][j]), "gk": _tile8(P_["ev_k_norm"][j])}
        else:
            d = {"w_in": np.ascontiguousarray(P_["od_w_in"][j]), "gq": _tile8(P_["od_q_norm"][j]), "gk": _tile8(P_["od_k_norm"][j])}
        d["g_mix"] = _gcol(P_["norm_mix"][l])
        return d

    even_keys = ("q", "k", "v", "qiw", "ki", "wi", "xa", "ga")
    odd_keys = ("q", "k", "v", "fl", "glu")
    base = inproj_inputs(0)
    ins = [dict(base, x=_shard(x, c), cs=_cs_table(c)) for c in range(NCORES)]
    res = run(build_tok(True, False, False, False), ins)
    o = {k: _cat(res, "o_" + k) for k in even_keys}
    _dbg("o0", o)
    for l in range(4):
        j = l // 2
        odd = (l % 2 == 1)
        att, oth = (_run_modd if odd else _run_meven)(o, j, P_)
        last = (l == 3)
        _dbg("att%d" % l, att)
        _dbg("oth%d" % l, oth)
        if DEBUG is not None and DEBUG.get("stop_after_mixer") == l:
            return None
        base = {"w_out": np.ascontiguousarray(P_["od_w_out" if odd else "ev_w_out"][j]), "w_gu": np.ascontiguousarray(P_["ffn_w_gu"][l]),
                "w_down": np.ascontiguousarray(P_["ffn_w_down"][l]), "g_ffn": _gcol(P_["norm_ffn"][l])}
        if odd:
            base["lng"] = _rows(P_["od_ln_g"][j])
            base["lnb"] = _rows(P_["od_ln_b"][j])
        if not last:
            base.update(inproj_inputs(l + 1))
        ins = []
        for c in range(NCORES):
            d = dict(base, x=_shard(x, c), att=_shard(att, c), oth=_shard(oth, c))
            if (not last) and odd:
                d["cs"] = _cs_table(c)
            ins.append(d)
        res = run(build_tok(False, last, (not odd), odd), ins)
        x = _cat(res, "x2")
        _dbg("x%d" % (l + 1), x)
        if not last:
            o = {k: _cat(res, "o_" + k) for k in (odd_keys if not odd else even_keys)}
            _dbg("o%d" % (l + 1), o)
        if DEBUG is not None and DEBUG.get("stop_after_layer") == l:
            return None
    return x.reshape(2, S_LEN, 1024).astype(np.float32)
```
